# Optimizing a Trainium2 kernel written in Bass

```python
import math, functools
import jax, jax.numpy as jnp
from jax import lax
import numpy as np

D_MODEL = 1024
BATCH = 8
SEQ = 2048
DEPTH = 2
DEC_BATCH = 32
DEC_SEQ = 8
PAST_LEN = 8192
PAGE_SIZE = 128

N_META = 16
DA_HEADS = 4
DA_HEAD_DIM = 64
DA_QK_WIDTH = DA_HEADS * 2 * DA_HEAD_DIM
DA_V_WIDTH = DA_HEADS * 2 * DA_HEAD_DIM
Q_BLOCK = 128
GDN_HEADS = 4
GDN_DK = 128
GDN_DV = 128
GDN_K_WIDTH = GDN_HEADS * GDN_DK
GDN_V_WIDTH = GDN_HEADS * GDN_DV
GDN_QKV_WIDTH = 2 * GDN_K_WIDTH + GDN_V_WIDTH
GDN_CONV = 4
GDN_CHUNK = 64
D_FF = 2816
FFN_CONV = 3
RMS_EPS = 1e-6
IN_WIDTHS = (DA_QK_WIDTH, DA_QK_WIDTH, DA_V_WIDTH, GDN_QKV_WIDTH, GDN_V_WIDTH, GDN_HEADS, GDN_HEADS, D_MODEL, D_MODEL)
D_IN = sum(IN_WIDTHS)
SPLIT_POINTS = tuple(int(s) for s in np.cumsum(IN_WIDTHS)[:-1])

kernel_name = 'hybrid_diffattn_gdn_convffn_step'


def rmsnorm(x, w):
    xf = x.astype(jnp.float32)
    y = xf * lax.rsqrt(jnp.mean(xf * xf, axis=-1, keepdims=True) + RMS_EPS)
    return (y * w.astype(jnp.float32)).astype(x.dtype)


def l2norm(x):
    xf = x.astype(jnp.float32)
    return xf * lax.rsqrt(jnp.sum(xf * xf, axis=-1, keepdims=True) + 1e-6)


def causal_dwconv(x, buf, w, b=None):
    width = w.shape[0]
    t = x.shape[1]
    xp = jnp.concatenate([buf.astype(x.dtype), x], axis=1)
    y = sum(xp[:, j:j + t] * w[j] for j in range(width))
    if b is not None:
        y = y + b
    return y, xp[:, xp.shape[1] - (width - 1):]


def diff_attn_prompt(q, k, v, lam):
    bsz, t, h, _, d = q.shape
    n_blk = -(-t // Q_BLOCK)
    pad = n_blk * Q_BLOCK - t
    qb = jnp.pad(q * (d ** -0.5), ((0, 0), (0, pad), (0, 0), (0, 0), (0, 0)))
    qb = jnp.moveaxis(qb.reshape(bsz, n_blk, Q_BLOCK, h, 2, d), 1, 0)
    vf = v.astype(jnp.float32)
    k_pos = jnp.arange(t)

    def one_block(args):
        q_blk, start = args
        q_pos = start + jnp.arange(Q_BLOCK)
        s = jnp.einsum('bqhmd,bkhmd->bhmqk', q_blk, k, preferred_element_type=jnp.float32)
        s = jnp.where(k_pos[None, :] <= q_pos[:, None], s, -jnp.inf)
        pr = jax.nn.softmax(s, axis=-1)
        a = pr[:, :, 0] - lam * pr[:, :, 1]
        return jnp.einsum('bhqk,bkhe->bqhe', a, vf)

    out = lax.map(one_block, (qb, jnp.arange(n_blk) * Q_BLOCK))
    out = jnp.moveaxis(out, 0, 1).reshape(bsz, n_blk * Q_BLOCK, h, 2 * d)[:, :t]
    return out.astype(q.dtype)


def diff_attn_sample(q, k, v, lam, k_past, v_past):
    t = q.shape[1]
    d = q.shape[-1]
    n_past = k_past.shape[1]
    qs = q * (d ** -0.5)
    s_past = jnp.einsum('bqhmd,bkhmd->bhmqk', qs, k_past, preferred_element_type=jnp.float32)
    s_new = jnp.einsum('bqhmd,bkhmd->bhmqk', qs, k, preferred_element_type=jnp.float32)
    s_new = jnp.where(jnp.tril(jnp.ones((t, t), bool)), s_new, -jnp.inf)
    pr = jax.nn.softmax(jnp.concatenate([s_past, s_new], axis=-1), axis=-1)
    a = pr[:, :, 0] - lam * pr[:, :, 1]
    out = (jnp.einsum('bhqk,bkhe->bqhe', a[..., :n_past], v_past.astype(jnp.float32))
           + jnp.einsum('bhqk,bkhe->bqhe', a[..., n_past:], v.astype(jnp.float32)))
    return out.astype(q.dtype)


def gated_delta_chunked(q, k, v, g, beta, state, chunk):
    bsz, t, h, dk = q.shape
    dv = v.shape[-1]
    n = t // chunk

    def blocks(a):
        a = a.reshape((bsz, n, chunk) + a.shape[2:])
        return jnp.moveaxis(jnp.moveaxis(a, 2, 3), 1, 0)

    qc, kc, vc, gc, bc = blocks(q), blocks(k), blocks(v), blocks(g), blocks(beta)
    gcum = jnp.cumsum(gc, axis=-1)
    causal = jnp.tril(jnp.ones((chunk, chunk), bool))
    strict = jnp.tril(jnp.ones((chunk, chunk), bool), -1)
    decay = jnp.exp(jnp.where(causal, gcum[..., :, None] - gcum[..., None, :], -jnp.inf))
    kb = kc * bc[..., None]
    lower = jnp.where(strict, jnp.einsum('nbhcd,nbhed->nbhce', kb, kc) * decay, 0.0)
    a_mat = lower + jnp.eye(chunk, dtype=jnp.float32)
    rhs = jnp.concatenate([vc * bc[..., None], kb * jnp.exp(gcum)[..., None]], axis=-1)
    sol = lax.linalg.triangular_solve(a_mat, rhs, left_side=True, lower=True, unit_diagonal=True)
    u, w = sol[..., :dv], sol[..., dv:]

    def step(s, xs):
        q_i, k_i, u_i, w_i, g_i, dec_i = xs
        v_new = u_i - jnp.einsum('bhcd,bhde->bhce', w_i, s)
        intra = jnp.einsum('bhcd,bhed->bhce', q_i, k_i) * dec_i
        o = (jnp.einsum('bhcd,bhde->bhce', q_i * jnp.exp(g_i)[..., None], s)
             + jnp.einsum('bhce,bhef->bhcf', intra, v_new))
        g_last = g_i[..., -1:]
        s = (s * jnp.exp(g_last)[..., None]
             + jnp.einsum('bhcd,bhce->bhde', k_i * jnp.exp(g_last - g_i)[..., None], v_new))
        return s, o

    state, o = lax.scan(step, state, (qc, kc, u, w, gcum, decay))
    o = jnp.moveaxis(jnp.moveaxis(o, 0, 1), 2, 3).reshape(bsz, t, h, dv)
    return o, state


def decoder_layer(x, p, lam_init, attn_fn, segments, gdn_conv_buf, gdn_state, ffn_conv_buf):
    f32 = jnp.float32
    bsz, t, _ = x.shape
    xn = rmsnorm(x, p['ln_mix_pre'])
    h = xn @ p['w_in']
    da_q, da_k, da_v, gdn_qkv, gdn_z, gdn_b, gdn_a, gate_a, gate_b = jnp.split(h, SPLIT_POINTS, axis=-1)

    q = da_q.reshape(bsz, t, DA_HEADS, 2, DA_HEAD_DIM)
    k = da_k.reshape(bsz, t, DA_HEADS, 2, DA_HEAD_DIM)
    v = da_v.reshape(bsz, t, DA_HEADS, 2 * DA_HEAD_DIM)
    lam = (jnp.exp(jnp.sum(p['lq1'].astype(f32) * p['lk1'].astype(f32)))
           - jnp.exp(jnp.sum(p['lq2'].astype(f32) * p['lk2'].astype(f32))) + lam_init)
    o_a = attn_fn(q, k, v, lam)
    o_a = (rmsnorm(o_a, p['da_subln']) * (1.0 - lam_init)).reshape(bsz, t, DA_V_WIDTH)

    qkv, new_gdn_conv = causal_dwconv(gdn_qkv, gdn_conv_buf, p['gdn_conv_w'])
    qkv = jax.nn.silu(qkv)
    gq, gk, gv = jnp.split(qkv, (GDN_K_WIDTH, 2 * GDN_K_WIDTH), axis=-1)
    gq = l2norm(gq.reshape(bsz, t, GDN_HEADS, GDN_DK)) * (GDN_DK ** -0.5)
    gk = l2norm(gk.reshape(bsz, t, GDN_HEADS, GDN_DK))
    gv = gv.reshape(bsz, t, GDN_HEADS, GDN_DV).astype(f32)
    beta = jax.nn.sigmoid(gdn_b.astype(f32))
    g = -jnp.exp(p['gdn_a_log'].astype(f32)) * jax.nn.softplus(gdn_a.astype(f32) + p['gdn_dt_bias'].astype(f32))
    s = gdn_state.astype(f32)
    outs = []
    start = 0
    for length, chunk in segments:
        sl = slice(start, start + length)
        o_seg, s = gated_delta_chunked(gq[:, sl], gk[:, sl], gv[:, sl], g[:, sl], beta[:, sl], s, chunk)
        outs.append(o_seg)
        start += length
    o_b = jnp.concatenate(outs, axis=1)
    o_b = rmsnorm(o_b, p['gdn_norm']) * jax.nn.silu(gdn_z.reshape(bsz, t, GDN_HEADS, GDN_DV).astype(f32))
    o_b = o_b.reshape(bsz, t, GDN_V_WIDTH).astype(x.dtype)

    mix = jax.nn.sigmoid(gate_a) * (o_a @ p['w_branch_a']) + jax.nn.sigmoid(gate_b) * (o_b @ p['w_branch_b'])
    x = x + rmsnorm(mix @ p['w_out'], p['ln_mix_post'])

    xn = rmsnorm(x, p['ln_ffn_pre'])
    up, gate = jnp.split(xn @ p['w_ffn_in'], 2, axis=-1)
    gate, new_ffn_conv = causal_dwconv(gate, ffn_conv_buf, p['ffn_conv_w'], p['ffn_conv_b'])
    x = x + rmsnorm((jax.nn.gelu(gate) * up) @ p['w_down'], p['ln_ffn_post'])
    return x, k, v, new_gdn_conv, s.astype(x.dtype), new_ffn_conv


def setup_inputs(seed: int = 0) -> dict:
    key = jax.random.key(seed)
    ks = iter(jax.random.split(key, 40))

    def nrm(shape, scale=1.0):
        return jax.random.normal(next(ks), shape, jnp.float32) * scale

    n_pages = PAST_LEN // PAGE_SIZE
    n_used = DEC_BATCH * n_pages
    n_pool = n_used + max(1, n_used // 4)
    page_table = jax.random.permutation(next(ks), n_pool)[:n_used].reshape(DEC_BATCH, n_pages).astype(jnp.int32)
    a_log = jnp.log(jax.random.uniform(next(ks), (DEPTH, GDN_HEADS), jnp.float32, 1.0, 16.0))
    dt = jnp.exp(jax.random.uniform(next(ks), (DEPTH, GDN_HEADS), jnp.float32, math.log(1e-3), math.log(1e-1)))
    dt_bias = dt + jnp.log(-jnp.expm1(-dt))
    return {
        'x_prompt': nrm((BATCH, SEQ, D_MODEL)),
        'x_sample': nrm((DEC_BATCH, DEC_SEQ, D_MODEL)),
        'cache_k': nrm((DEPTH, n_pool, PAGE_SIZE, DA_HEADS, 2, DA_HEAD_DIM)),
        'cache_v': nrm((DEPTH, n_pool, PAGE_SIZE, DA_HEADS, 2 * DA_HEAD_DIM)),
        'page_table': page_table,
        'state_gdn': nrm((DEPTH, DEC_BATCH, GDN_HEADS, GDN_DK, GDN_DV), 0.1),
        'state_gdn_conv': nrm((DEPTH, DEC_BATCH, GDN_CONV - 1, GDN_QKV_WIDTH)),
        'state_ffn_conv': nrm((DEPTH, DEC_BATCH, FFN_CONV - 1, D_FF)),
        'meta_tokens': nrm((N_META, D_MODEL)),
        'ln_mix_pre': 1.0 + nrm((DEPTH, D_MODEL), 0.02),
        'ln_mix_post': 1.0 + nrm((DEPTH, D_MODEL), 0.02),
        'ln_ffn_pre': 1.0 + nrm((DEPTH, D_MODEL), 0.02),
        'ln_ffn_post': 1.0 + nrm((DEPTH, D_MODEL), 0.02),
        'w_in': nrm((DEPTH, D_MODEL, D_IN), D_MODEL ** -0.5),
        'da_lambda_q1': nrm((DEPTH, DA_HEAD_DIM), 0.1),
        'da_lambda_k1': nrm((DEPTH, DA_HEAD_DIM), 0.1),
        'da_lambda_q2': nrm((DEPTH, DA_HEAD_DIM), 0.1),
        'da_lambda_k2': nrm((DEPTH, DA_HEAD_DIM), 0.1),
        'da_subln': 1.0 + nrm((DEPTH, 2 * DA_HEAD_DIM), 0.02),
        'gdn_conv_w': nrm((DEPTH, GDN_CONV, GDN_QKV_WIDTH), GDN_CONV ** -0.5),
        'gdn_a_log': a_log,
        'gdn_dt_bias': dt_bias,
        'gdn_norm': 1.0 + nrm((DEPTH, GDN_DV), 0.02),
        'w_branch_a': nrm((DEPTH, DA_V_WIDTH, D_MODEL), DA_V_WIDTH ** -0.5),
        'w_branch_b': nrm((DEPTH, GDN_V_WIDTH, D_MODEL), GDN_V_WIDTH ** -0.5),
        'w_out': nrm((DEPTH, D_MODEL, D_MODEL), D_MODEL ** -0.5),
        'w_ffn_in': nrm((DEPTH, D_MODEL, 2 * D_FF), D_MODEL ** -0.5),
        'ffn_conv_w': nrm((DEPTH, FFN_CONV, D_FF), FFN_CONV ** -0.5),
        'ffn_conv_b': nrm((DEPTH, D_FF), 0.01),
        'w_down': nrm((DEPTH, D_FF, D_MODEL), D_FF ** -0.5),
    }


def reference(x_prompt, x_sample, cache_k, cache_v, page_table, state_gdn, state_gdn_conv, state_ffn_conv,
              meta_tokens, ln_mix_pre, ln_mix_post, ln_ffn_pre, ln_ffn_post, w_in,
              da_lambda_q1, da_lambda_k1, da_lambda_q2, da_lambda_k2, da_subln,
              gdn_conv_w, gdn_a_log, gdn_dt_bias, gdn_norm, w_branch_a, w_branch_b, w_out,
              w_ffn_in, ffn_conv_w, ffn_conv_b, w_down):
    bp, seq, _ = x_prompt.shape
    ds, dseq, _ = x_sample.shape
    dt = x_prompt.dtype
    meta = jnp.broadcast_to(meta_tokens[None].astype(dt), (bp, N_META, D_MODEL))
    xp = jnp.concatenate([meta, x_prompt], axis=1)
    xs = x_sample
    prompt_segments = ((N_META, N_META), (seq, GDN_CHUNK))
    sample_segments = ((dseq, math.gcd(dseq, GDN_CHUNK)),)
    zero_gdn_conv = jnp.zeros((bp, GDN_CONV - 1, GDN_QKV_WIDTH), dt)
    zero_gdn = jnp.zeros((bp, GDN_HEADS, GDN_DK, GDN_DV), jnp.float32)
    zero_ffn_conv = jnp.zeros((bp, FFN_CONV - 1, D_FF), dt)
    kp_l, vp_l, ks_l, vs_l, sp_l, ss_l, cp_l, cs_l, fp_l, fs_l = ([] for _ in range(10))
    for l in range(DEPTH):
        p = {
            'ln_mix_pre': ln_mix_pre[l], 'ln_mix_post': ln_mix_post[l],
            'ln_ffn_pre': ln_ffn_pre[l], 'ln_ffn_post': ln_ffn_post[l], 'w_in': w_in[l],
            'lq1': da_lambda_q1[l], 'lk1': da_lambda_k1[l], 'lq2': da_lambda_q2[l], 'lk2': da_lambda_k2[l],
            'da_subln': da_subln[l], 'gdn_conv_w': gdn_conv_w[l], 'gdn_a_log': gdn_a_log[l],
            'gdn_dt_bias': gdn_dt_bias[l], 'gdn_norm': gdn_norm[l], 'w_branch_a': w_branch_a[l],
            'w_branch_b': w_branch_b[l], 'w_out': w_out[l], 'w_ffn_in': w_ffn_in[l],
            'ffn_conv_w': ffn_conv_w[l], 'ffn_conv_b': ffn_conv_b[l], 'w_down': w_down[l],
        }
        lam_init = 0.8 - 0.6 * math.exp(-0.3 * l)
        xp, k_new, v_new, c_new, s_new, f_new = decoder_layer(
            xp, p, lam_init, diff_attn_prompt, prompt_segments, zero_gdn_conv, zero_gdn, zero_ffn_conv)
        kp_l.append(k_new); vp_l.append(v_new); cp_l.append(c_new); sp_l.append(s_new); fp_l.append(f_new)
        k_past = cache_k[l][page_table].reshape(ds, -1, DA_HEADS, 2, DA_HEAD_DIM)
        v_past = cache_v[l][page_table].reshape(ds, -1, DA_HEADS, 2 * DA_HEAD_DIM)
        attn_s = functools.partial(diff_attn_sample, k_past=k_past, v_past=v_past)
        xs, k_new, v_new, c_new, s_new, f_new = decoder_layer(
            xs, p, lam_init, attn_s, sample_segments, state_gdn_conv[l], state_gdn[l], state_ffn_conv[l])
        ks_l.append(k_new); vs_l.append(v_new); cs_l.append(c_new); ss_l.append(s_new); fs_l.append(f_new)
    return (xp[:, N_META:], xs,
            jnp.stack(kp_l), jnp.stack(vp_l), jnp.stack(ks_l), jnp.stack(vs_l),
            jnp.stack(sp_l), jnp.stack(ss_l), jnp.stack(cp_l), jnp.stack(cs_l),
            jnp.stack(fp_l), jnp.stack(fs_l))
```

```python
import os
import math
import numpy as np
from contextlib import ExitStack
import concourse.bass as bass
import concourse.mybir as mybir
from concourse.bass_utils import run_bass_kernel_spmd

F32 = mybir.dt.float32
BF16 = mybir.dt.bfloat16
I32 = mybir.dt.int32
AF = mybir.ActivationFunctionType
ALU = mybir.AluOpType
AX = mybir.AxisListType

ENGS = ['pe', 'act', 'dve', 'pool', 'sp']
ATTR = {'pe': 'tensor', 'act': 'scalar', 'dve': 'vector', 'pool': 'gpsimd', 'sp': 'sync'}
NS = 8

NCORE = 8
D = 1024
KC = 8
SEQ = 2048
NMETA = 16
DSEQ = 8
NSQ = 4
NPG = 64
DIN = 5640
DFF = 2816
NFB = 22
CH = 512
NPCH = 4
EPS = 1e-6
NEG = -1.0e5


class Tl:
    __slots__ = ('w', 'r', 'const', 'excl')

    def __init__(self, const=False, excl=False):
        self.w = None
        self.r = []
        self.const = const
        self.excl = excl


class Op:
    __slots__ = ('eng', 'fn', 'deps', 'dma', 'sig', 'val', 'sem', 'idx')


def _l(x):
    if x is None:
        return []
    if isinstance(x, Tl):
        return [x]
    out = []
    for y in x:
        out.extend(_l(y))
    return out


class Prog:
    def __init__(self, nc):
        self.nc = nc
        self.ops = {e: [] for e in ENGS}
        self.hist = {e: [] for e in ENGS}

    def op(self, eng, fn, reads=(), writes=(), dma=False):
        reads = _l(reads)
        writes = _l(writes)
        ex = [t for t in reads if t.excl and t not in writes]
        if ex:
            reads = [t for t in reads if not t.excl]
            writes = writes + ex
        o = Op()
        o.eng, o.fn, o.dma, o.deps, o.sig = eng, fn, dma, [], False
        o.val = 0
        o.sem = None
        o.idx = 0
        deps = o.deps

        def add(p):
            if p is None or p is o:
                return
            if p not in deps:
                deps.append(p)
                p.sig = True
        for t in reads:
            p = t.w
            if p is not None:
                if not (p.eng == 'pe' and eng == 'pe' and not p.dma and not dma):
                    add(p)
        for t in writes:
            p = t.w
            if p is not None and (dma or p.dma or p.eng != eng):
                add(p)
            for rd in t.r:
                if dma or rd.dma or rd.eng != eng:
                    add(rd)
        for t in reads:
            if not t.const:
                t.r.append(o)
        for t in writes:
            t.w = o
            t.r = []
        if dma:
            h = self.hist[eng]
            k = len(h)
            if k >= NS:
                add(h[k - NS])
            h.append(o)
            o.idx = k
        self.ops[eng].append(o)
        return o

    def emit(self, stack):
        nc = self.nc
        semh = {}
        for e in ENGS:
            cnt = 0
            for o in self.ops[e]:
                if o.dma:
                    o.sem = ('d', e, o.idx % NS)
                    o.val = 16 * (o.idx // NS + 1)
                elif o.sig:
                    cnt += 1
                    o.val = cnt
                    o.sem = ('c', e)
                if o.sem is not None and o.sem not in semh:
                    semh[o.sem] = stack.enter_context(nc.semaphore('s_' + '_'.join(map(str, o.sem))))
        block = stack.enter_context(nc.Block())
        for e in ENGS:
            ops = self.ops[e]
            if not ops:
                continue

            def body(h, ops=ops, e=e):
                known = {}
                for o in ops:
                    for p in o.deps:
                        if known.get(p.sem, 0) < p.val:
                            h.wait_ge(semh[p.sem], p.val)
                            known[p.sem] = p.val
                    ins = o.fn(h)
                    if o.dma:
                        ins.then_inc(semh[o.sem], 16)
                    elif o.sig:
                        ins.then_inc(semh[o.sem], 1)
                last = {}
                for o in self.hist[e]:
                    last[o.sem] = o.val
                for sname, v in last.items():
                    if known.get(sname, 0) < v:
                        h.wait_ge(semh[sname], v)
            getattr(block, ATTR[e])(body)


class Ring:
    def __init__(self, items):
        self.items = items
        self.i = 0

    def next(self):
        x = self.items[self.i % len(self.items)]
        self.i += 1
        return x


class RG:
    def __init__(self, kind, idx, col0, nr):
        self.kind, self.idx, self.col0, self.nr = kind, idx, col0, nr


class Chunk:
    def __init__(self, kind, pi):
        self.kind, self.pi = kind, pi
        if kind == 'S':
            self.n = NMETA + NSQ * DSEQ
            self.rgs = [RG('meta', 0, 0, NMETA)] + [RG('samp', j, NMETA + DSEQ * j, DSEQ) for j in range(NSQ)]
        else:
            self.n = CH
            self.rgs = [RG('pt', 4 * pi + r, 128 * r, 128) for r in range(4)]


def build(stop_after=None, npool=2560):
    nc = bass.Bass("TRN2", target_bir_lowering=False)

    def din(name, shape, dt=F32):
        return nc.dram_tensor(name, list(shape), dt, kind="ExternalInput").ap()

    def dout(name, shape, dt=F32):
        return nc.dram_tensor(name, list(shape), dt, kind="ExternalOutput").ap()

    xp = din("xp", [SEQ, D])
    xs = din("xs", [NSQ, DSEQ, D])
    ck = din("ck", [2, npool, 128, 512])
    cv = din("cv", [2, npool, 128, 512])
    ptab = din("pt", [1, NSQ * NPG], I32)
    sgd = din("sg", [2, NSQ, 4, 128, 128])
    sgc = din("sgc", [2, NSQ, 3, 1536])
    sfc = din("sfc", [2, NSQ, 2, DFF])
    meta = din("meta", [NMETA, D])
    lnw = [din(n, [2, D]) for n in ("ln_mix_pre", "ln_mix_post", "ln_ffn_pre", "ln_ffn_post")]
    w_in = din("w_in", [2, D, DIN])
    lamp = [din(n, [2, 64]) for n in ("lq1", "lk1", "lq2", "lk2")]
    subln = din("da_subln", [2, 128])
    gcw = din("gdn_conv_w", [2, 4, 1536])
    alog = din("gdn_a_log", [2, 4])
    dtbias = din("gdn_dt_bias", [2, 4])
    gnorm = din("gdn_norm", [2, 128])
    wbra = din("w_branch_a", [2, 512, D])
    wbrb = din("w_branch_b", [2, 512, D])
    w_out = din("w_out", [2, D, D])
    w_ffn = din("w_ffn_in", [2, D, 2 * DFF])
    fcw = din("ffn_conv_w", [2, 3, DFF])
    fcb = din("ffn_conv_b", [2, DFF])
    w_dn = din("w_down", [2, DFF, D])

    yp = dout("yp", [SEQ, D])
    ys = dout("ys", [NSQ, DSEQ, D])
    nkp = dout("nkp", [2, NMETA + SEQ, 512])
    nvp = dout("nvp", [2, NMETA + SEQ, 512])
    nks = dout("nks", [2, NSQ, DSEQ, 512])
    nvs = dout("nvs", [2, NSQ, DSEQ, 512])
    ngp = dout("ngp", [2, 4, 128, 128])
    ngs = dout("ngs", [2, NSQ, 4, 128, 128])
    ncp = dout("ncp", [2, 3, 1536])
    ncs = dout("ncs", [2, NSQ, 3, 1536])
    nfp = dout("nfp", [2, 2, DFF])
    nfs = dout("nfs", [2, NSQ, 2, DFF])

    st = ExitStack()
    with st:
        P = Prog(nc)

        def sb(name, shape, dt):
            return st.enter_context(nc.sbuf_tensor(name, list(shape), dt))

        def psum(name, shape, dt):
            return st.enter_context(nc.psum_tensor(name, list(shape), dt))

        def MM(ot, o, lt, l, rt, r, start=True, stop=True, sgc=False):
            if sgc:
                P.op('pe', lambda h: h.matmul(o, lhsT=l, rhs=r, start=start, stop=stop, skip_group_check=True), reads=_l(lt) + _l(rt), writes=ot)
            else:
                P.op('pe', lambda h: h.matmul(o, lhsT=l, rhs=r, start=start, stop=stop), reads=_l(lt) + _l(rt), writes=ot)

        def ACT(ot, o, it, i, func, bias=None, scale=1.0, bt=None, eng='act'):
            kw = {}
            if bias is not None:
                kw['bias'] = bias
            P.op('act', lambda h: h.activation(out=o, in_=i, func=func, scale=scale, **kw), reads=_l(it) + _l(bt), writes=ot)

        def TS(eng, ot, o, it, i, s1, s2, op0, op1=None, st_=None):
            if op1 is None:
                P.op(eng, lambda h: h.tensor_scalar(o, i, s1, None, op0=op0), reads=_l(it) + _l(st_), writes=ot)
            else:
                P.op(eng, lambda h: h.tensor_scalar(o, i, s1, s2, op0=op0, op1=op1), reads=_l(it) + _l(st_), writes=ot)

        def STT(eng, ot, o, i0t, i0, sc, i1t, i1, op0, op1, st_=None):
            P.op(eng, lambda h: h.scalar_tensor_tensor(out=o, in0=i0, scalar=sc, in1=i1, op0=op0, op1=op1), reads=_l(i0t) + _l(i1t) + _l(st_), writes=ot)

        def TT(eng, ot, o, i0t, i0, i1t, i1, op):
            P.op(eng, lambda h: h.tensor_tensor(out=o, in0=i0, in1=i1, op=op), reads=_l(i0t) + _l(i1t), writes=ot)

        def CP(eng, ot, o, it, i):
            if eng == 'act':
                P.op('act', lambda h: h.copy(o, i), reads=it, writes=ot)
            else:
                P.op(eng, lambda h: h.tensor_copy(o, i), reads=it, writes=ot)

        def MS(eng, ot, o, val):
            P.op(eng, lambda h: h.memset(o, val), writes=ot)

        def DMA(eng, o, i, reads=None, writes=None):
            P.op(eng, lambda h: h.dma_start(out=o, in_=i), reads=reads, writes=writes, dma=True)

        ident32 = sb("ident32", [128, 128], F32)
        identb = sb("identb", [128, 128], BF16)
        ones_b = sb("ones_b", [128, 128], BF16)
        ones32 = sb("ones32", [128, 128], F32)
        tri32 = sb("tri32", [128, 128], F32)
        triu_b = sb("triu_b", [128, 128], BF16)
        maskU = sb("maskU", [128, 128], F32)
        maskLS = sb("maskLS", [128, 128], F32)
        sel16 = sb("sel16", [16, 8], F32)
        cm8 = sb("cm8", [8, 16], BF16)
        m0c = sb("m0c", [16, 1], F32)
        m1c = sb("m1c", [16, 1], F32)
        CT = Tl()

        def asel(o, pattern, cmp_, fill, base, cm):
            P.op('pool', lambda h: h.affine_select(out=o, in_=o, pattern=pattern, compare_op=cmp_, fill=fill, base=base, channel_multiplier=cm), reads=CT, writes=CT)

        MS('pool', CT, ident32[:], 0.0)
        asel(ident32[:], [[-1, 128]], ALU.not_equal, 1.0, 0, 1)
        MS('pool', CT, ones32[:], 1.0)
        MS('pool', CT, ones_b[:], 1.0)
        MS('pool', CT, tri32[:], 1.0)
        asel(tri32[:], [[1, 128]], ALU.is_ge, 0.0, 0, -1)
        MS('pool', CT, maskU[:], 0.0)
        asel(maskU[:], [[1, 128]], ALU.is_ge, NEG, 0, -1)
        MS('pool', CT, maskLS[:], 0.0)
        asel(maskLS[:], [[-1, 128]], ALU.is_gt, NEG, 0, 1)
        MS('pool', CT, sel16[:], 0.0)
        asel(sel16[:], [[-1, 8]], ALU.not_equal, 1.0, 0, 1)
        asel(sel16[:], [[-1, 8]], ALU.not_equal, 1.0, -8, 1)
        MS('pool', CT, cm8[:], 1.0)
        asel(cm8[:], [[0, 2], [1, 8]], ALU.is_ge, 0.0, 0, -1)
        MS('pool', CT, m0c[:], 1.0)
        asel(m0c[:], [[0, 1]], ALU.is_gt, 0.0, 8, -1)
        MS('pool', CT, m1c[:], 1.0)
        asel(m1c[:], [[0, 1]], ALU.is_ge, 0.0, -8, 1)
        P.op('pool', lambda h: h.tensor_copy(identb[:], ident32[:]), reads=CT, writes=CT)
        P.op('pool', lambda h: h.tensor_copy(triu_b[:], tri32[:]), reads=CT, writes=CT)

        CT.const = True
        banks = [psum(f"bank{i}", [128, 512], F32) for i in range(8)]
        bankT = [Tl(excl=True) for _ in range(8)]
        big = Ring([0, 1, 2])
        qring = Ring([6, 7])
        gring = Ring([4, 5, 6, 7])
        halfring = Ring([4, 5])

        def nextq():
            b_ = qring.next()
            return bankT[b_], banks[b_][:, 0:128]

        def nextg():
            b_ = gring.next()
            return bankT[b_], banks[b_][:, 0:128]

        def nexthalf():
            b_ = halfring.next()
            return bankT[b_], banks[b_][:, 0:256]

        xT = sb("xT", [128, KC, CH], F32)
        xTt = [Tl() for _ in range(KC)]
        xnT = sb("xnT", [128, KC, CH], BF16)
        xnTt = [Tl() for _ in range(KC)]
        NSLOT = 38
        arena = sb("arena", [128, NSLOT * 512], BF16)
        arena32 = arena.bitcast(F32)
        slotT = [Tl() for _ in range(NSLOT)]

        def slot(k, n=CH):
            return arena[:, k * 512:k * 512 + n]

        def slot32(k, n):
            return arena32[:, k * 256:k * 256 + n]

        def slots(k, cnt):
            return slotT[k:k + cnt]

        tokst = arena32[:, 22 * 256:22 * 256 + 3072]
        tokstT = slotT[22:34]
        tokg = arena32[:, 28 * 256:28 * 256 + 1536]
        tokgT = slotT[28:34]

        NW = 2
        wbuf = [sb(f"wbuf{i}", [128, KC, 512], BF16) for i in range(NW)]
        wbufT = [Tl() for _ in range(NW)]
        wring = Ring(list(range(NW)))
        wba_t = sb("wba", [128, KC, 8], BF16)
        wbaT = Tl()

        NT = 17
        KTs = [sb(f"KT{l}", [128, 4, NMETA + SEQ], BF16) for l in range(2)]
        KTt = [[Tl() for _ in range(NT)] for l in range(2)]
        Vts = [sb(f"Vt{l}", [128, NT, 4, 130], BF16) for l in range(2)]
        Vtt = [[Tl() for _ in range(NT)] for l in range(2)]
        for l in range(2):
            MS('pool', Vtt[l], Vts[l][:], 1.0)
        Sst = [sb(f"Sst{l}", [128, 4, 128], F32) for l in range(2)]
        Sbb = [sb(f"Sbb{l}", [128, 4, 128], BF16) for l in range(2)]
        SstT = [[Tl() for _ in range(4)] for l in range(2)]
        SbbT = [[Tl() for _ in range(4)] for l in range(2)]
        for l in range(2):
            MS('pool', SstT[l], Sst[l][:], 0.0)
            MS('pool', SbbT[l], Sbb[l][:], 0.0)
        SsS = [sb(f"SsS{i}", [128, 4, 128], F32) for i in range(1)]
        SsB = [sb(f"SsB{i}", [128, 4, 128], BF16) for i in range(1)]
        SsST = [[Tl() for _ in range(4)] for i in range(1)]
        SsBT = [[Tl() for _ in range(4)] for i in range(1)]
        halo_g = [sb(f"halog{l}", [128, 12, 3], F32) for l in range(2)]
        halo_gT = [[Tl() for _ in range(12)] for l in range(2)]
        halo_f = [sb(f"halof{l}", [128, NFB, 2], F32) for l in range(2)]
        halo_fT = [[Tl() for _ in range(NFB)] for l in range(2)]
        for l in range(2):
            MS('pool', halo_gT[l], halo_g[l][:], 0.0)
            MS('pool', halo_fT[l], halo_f[l][:], 0.0)
        hgs = sb("hgs", [128, 12, NSQ, 3], F32)
        hgsT = Tl()
        hgo = sb("hgo", [128, 12, NSQ, 3], F32)
        hgoT = Tl()
        hfs = sb("hfs", [128, NFB, NSQ, 2], F32)
        hfsT = Tl()
        hfo = sb("hfo", [128, NFB, NSQ, 2], F32)
        hfoT = Tl()

        NPRE = 2
        pre = [sb(f"pre{i}", [128, 3 + CH], F32) for i in range(NPRE)]
        preT = [Tl() for _ in range(NPRE)]
        prering = Ring(list(range(NPRE)))
        cacc = [sb(f"cacc{i}", [128, CH], F32) for i in range(2)]
        caccT = [Tl() for _ in range(2)]
        caccring = Ring([0, 1])
        ytmp = [sb(f"ytmp{i}", [128, CH], F32) for i in range(2)]
        ytmpT = [Tl() for _ in range(2)]
        ytring = Ring([0, 1])
        sqb = [sb(f"sqb{i}", [128, CH], BF16) for i in range(2)]
        sqbT = [Tl() for _ in range(2)]
        sqring = Ring([0, 1])
        rstd = [sb(f"rstd{i}", [128, CH], F32) for i in range(1)]
        rstdT = [Tl() for _ in range(1)]
        rsring = Ring([0])

        cstA = [sb(f"cstA{l}", [128, 104], F32) for l in range(2)]
        cstB = [sb(f"cstB{l}", [128, 66], F32) for l in range(2)]
        lamv = [sb(f"lamv{l}", [128, 4], F32) for l in range(2)]
        negA = [sb(f"negA{l}", [128, 4], F32) for l in range(2)]
        dtb = [sb(f"dtb{l}", [128, 4], F32) for l in range(2)]
        sgn16 = [sb(f"sgn{l}", [16, 1], F32) for l in range(2)]
        cstg = ytmp[1][:, 0:128]
        cstgT = ytmpT[1]
        lamt = ytmp[0][:, 0:256].rearrange("p (a b) -> p a b", a=4)
        lamtT = ytmpT[0]
        LCT = [Tl() for _ in range(2)]

        def lam_init(l):
            return 0.8 - 0.6 * math.exp(-0.3 * l)

        for l in range(2):
            for i, w in enumerate(lnw):
                DMA('sp', cstg[8 * i:8 * i + 8, :], w[l].rearrange("(r c) -> r c", c=128), writes=cstgT)
            DMA('sp', cstg[32:80, :], gcw[l].rearrange("j (b c) -> (j b) c", c=128), writes=cstgT)
            DMA('sp', cstg[80:81, :], subln[l:l + 1, :], writes=cstgT)
            DMA('sp', cstg[81:82, :], gnorm[l:l + 1, :], writes=cstgT)
            DMA('sp', cstg[82:104, :], fcb[l].rearrange("(r c) -> r c", c=128), writes=cstgT)
            bk = big.next()
            MM(bankT[bk], banks[bk][:, 0:104], cstgT, cstg[0:104, :], CT, ident32[0:104, 0:104])
            CP('dve', LCT[l], cstA[l][:], bankT[bk], banks[bk][:, 0:104])
            DMA('sp', cstg[0:66, :], fcw[l].rearrange("j (b c) -> (j b) c", c=128), writes=cstgT)
            bk = big.next()
            MM(bankT[bk], banks[bk][:, 0:66], cstgT, cstg[0:66, :], CT, ident32[0:66, 0:66])
            CP('dve', LCT[l], cstB[l][:], bankT[bk], banks[bk][:, 0:66])
            TS('dve', LCT[l], cstA[l][:, 0:32], LCT[l], cstA[l][:, 0:32], math.sqrt(D), None, ALU.mult)
            TS('dve', LCT[l], cstA[l][:, 80:81], LCT[l], cstA[l][:, 80:81], 1.0 - lam_init(l), None, ALU.mult)
            for i, w in enumerate(lamp):
                DMA('sp', lamt[:, i, :], w[l:l + 1, :].to_broadcast([128, 64]), writes=lamtT)
            TT('dve', lamtT, lamt[:, 0, :], lamtT, lamt[:, 0, :], lamtT, lamt[:, 1, :], ALU.mult)
            TT('dve', lamtT, lamt[:, 2, :], lamtT, lamt[:, 2, :], lamtT, lamt[:, 3, :], ALU.mult)
            P.op('dve', lambda h, l=l: h.reduce_sum(lamv[l][:, 2:3], lamt[:, 0, :], axis=AX.X), reads=lamtT, writes=LCT[l])
            P.op('dve', lambda h, l=l: h.reduce_sum(lamv[l][:, 3:4], lamt[:, 2, :], axis=AX.X), reads=lamtT, writes=LCT[l])
            ACT(LCT[l], lamv[l][:, 2:4], LCT[l], lamv[l][:, 2:4], AF.Exp)
            TT('dve', LCT[l], lamv[l][:, 0:1], LCT[l], lamv[l][:, 2:3], LCT[l], lamv[l][:, 3:4], ALU.subtract)
            TS('dve', LCT[l], lamv[l][:, 0:1], LCT[l], lamv[l][:, 0:1], lam_init(l), None, ALU.add)
            TS('dve', LCT[l], lamv[l][:, 1:2], LCT[l], lamv[l][:, 0:1], -1.0, None, ALU.mult)
            STT('dve', LCT[l], sgn16[l][:], CT, m1c[:], lamv[l][0:16, 1:2], CT, m0c[:], ALU.mult, ALU.add, st_=LCT[l])
            DMA('sp', negA[l][:], alog[l:l + 1, :].to_broadcast([128, 4]), writes=LCT[l])
            DMA('sp', dtb[l][:], dtbias[l:l + 1, :].to_broadcast([128, 4]), writes=LCT[l])
            ACT(LCT[l], negA[l][:], LCT[l], negA[l][:], AF.Exp)
            TS('dve', LCT[l], negA[l][:], LCT[l], negA[l][:], -1.0, None, ALU.mult)

        for l in range(2):
            LCT[l].const = True

        def lnw32(l, which, kc):
            return cstA[l][:, 8 * which + kc:8 * which + kc + 1]

        ptb = sb("ptb", [128, NSQ * NPG], I32)
        iot = sb("iot", [128, 1], I32)
        iof = sb("iof", [128, 2], F32)
        pidx = [sb(f"pidx{l}", [128, NSQ * NPG], I32) for l in range(2)]
        ptsT = Tl()
        DMA('sp', ptb[:, :], ptab.to_broadcast([128, NSQ * NPG]), writes=ptsT)
        P.op('pool', lambda h: h.iota(iot[:, :], pattern=[[0, 1]], base=0, channel_multiplier=1), writes=ptsT)
        P.op('pool', lambda h: h.tensor_copy(iof[:, 0:1], iot[:, :]), reads=ptsT, writes=ptsT)
        P.op('pool', lambda h: h.tensor_scalar(iof[:, 1:2], iof[:, 0:1], float(npool * 128), None, op0=ALU.add), reads=ptsT, writes=ptsT)
        for l in range(2):
            P.op('pool', lambda h, l=l: h.tensor_scalar(pidx[l][:, :], ptb[:, :], 128, iof[:, l:l + 1], op0=ALU.mult, op1=ALU.add), reads=ptsT, writes=ptsT)
        ck_rows = ck.rearrange("l n p f -> (l n p) f")
        cv_rows = cv.rearrange("l n p f -> (l n p) f")

        def load_strip(src2d, nk, ncols):
            wi = wring.next()
            DMA('pool', wbuf[wi][:, 0:nk, 0:ncols], src2d.rearrange("(k p) n -> p k n", p=128), writes=wbufT[wi])
            return wi

        def proj_feat(wi, blk, nk, srcf, n, bk=None, bw=128):
            if bk is None:
                bk = big.next()
            for kc in range(nk):
                t, ap = srcf(kc)
                MM(bankT[bk], banks[bk][0:bw, 0:n], wbufT[wi], wbuf[wi][:, kc, blk * 128:blk * 128 + bw], t, ap, start=(kc == 0), stop=(kc == nk - 1))
            return bk

        def xn_src(n):
            return lambda kc: (xnTt[kc], xnT[:, kc, 0:n])

        def rms_stats(srcf, n, nkc=KC):
            bk = 3
            for kc in range(nkc):
                t, ap = srcf(kc)
                si = sqring.next()
                ACT(sqbT[si], sqb[si][:, 0:n], t, ap, AF.Square)
                MM(bankT[bk], banks[bk][:, 0:n], CT, ones_b[:], sqbT[si], sqb[si][:, 0:n], start=(kc == 0), stop=(kc == nkc - 1))
            ri = rsring.next()
            ACT(rstdT[ri], rstd[ri][:, 0:n], bankT[bk], banks[bk][:, 0:n], AF.Sqrt, bias=D * EPS)
            P.op('dve', lambda h: h.reciprocal(rstd[ri][:, 0:n], rstd[ri][:, 0:n]), reads=rstdT[ri], writes=rstdT[ri])
            return ri

        def prenorm(l, which, n):
            ri = rms_stats(lambda kc: (xTt[kc], xT[:, kc, 0:n]), n)
            for kc in range(KC):
                STT('dve', xnTt[kc], xnT[:, kc, 0:n], xTt[kc], xT[:, kc, 0:n], lnw32(l, which, kc), rstdT[ri], rstd[ri][:, 0:n], ALU.mult, ALU.mult, st_=LCT[l])

        YB = 22

        def ybuf(ob, n):
            return slots(YB + 2 * ob, 2), slot32(YB + 2 * ob, n)

        def postnorm_residual(l, which, n):
            ri = rms_stats(lambda kc: ybuf(kc, n), n)
            for ob in range(KC):
                yt, ya = ybuf(ob, n)
                STT('dve', yt, ya, yt, ya, lnw32(l, which, ob), rstdT[ri], rstd[ri][:, 0:n], ALU.mult, ALU.mult, st_=LCT[l])
                TT('pool', xTt[ob], xT[:, ob, 0:n], xTt[ob], xT[:, ob, 0:n], yt, ya, ALU.add)

        def norm_transpose(ot, oap, nq, gcol, gt, dst_t, dst_ap, extra=None):
            yi = ytring.next()
            ACT(ytmpT[yi], ytmp[yi][0:nq, 0:128], ot, oap, AF.Square)
            P.op('dve', lambda h: h.reduce_sum(ytmp[yi][0:nq, 128:129], ytmp[yi][0:nq, 0:128], axis=AX.X), reads=ytmpT[yi], writes=ytmpT[yi])
            ACT(ytmpT[yi], ytmp[yi][0:nq, 128:129], ytmpT[yi], ytmp[yi][0:nq, 128:129], AF.Sqrt, bias=EPS, scale=1.0 / 128.0)
            P.op('dve', lambda h: h.reciprocal(ytmp[yi][0:nq, 128:129], ytmp[yi][0:nq, 128:129]), reads=ytmpT[yi], writes=ytmpT[yi])
            TS('dve', ytmpT[yi], ytmp[yi][0:nq, 256:384], _l(ot) + [ytmpT[yi]], oap, ytmp[yi][0:nq, 128:129], None, ALU.mult)
            qt_, qa = nextq()
            MM(qt_, qa[:, 0:nq], ytmpT[yi], ytmp[yi][0:nq, 256:384], CT, ident32[0:nq, 0:nq])
            if extra is None:
                TS('dve', dst_t, dst_ap, qt_, qa[:, 0:nq], gcol, None, ALU.mult, st_=gt)
            else:
                et, ea = extra
                STT('dve', dst_t, dst_ap, qt_, qa[:, 0:nq], gcol, et, ea, ALU.mult, ALU.mult, st_=gt)

        PTb = [sb(f"PTb{i}", [128, 256], BF16) for i in range(2)]
        PTbT = [Tl() for _ in range(2)]
        ptring = Ring([0, 1])
        att = [sb(f"att{i}", [128, 264], F32) for i in range(1)]
        attT = [Tl() for _ in range(1)]
        attring = Ring([0])
        NKP = 2
        kpg = [sb(f"kpg{i}", [128, 512], BF16) for i in range(NKP)]
        kpgT = [Tl() for _ in range(NKP)]
        vpg = [sb(f"vpg{i}", [128, 512], BF16) for i in range(NKP)]
        vpgT = [Tl() for _ in range(NKP)]
        pgring = Ring(list(range(NKP)))
        KpT = [sb(f"KpT{i}", [128, 512], BF16) for i in range(2)]
        KpTT = [Tl() for _ in range(2)]
        kptring = Ring([0, 1])
        PTs = [sb(f"PTs{i}", [128, 64], BF16) for i in range(1)]
        PTsT = [Tl() for _ in range(1)]
        ptsring = Ring([0])
        Qblk = sb("Qblk", [128, NSQ, 4, 16], BF16)
        QblkT = Tl()
        MS('pool', QblkT, Qblk[:], 0.0)
        KnT = sb("KnT", [128, 4, NSQ * DSEQ], BF16)
        KnTT = Tl()
        VnRaw = arena[:, 28 * 512:28 * 512 + NSQ * 4 * 130]
        Vn = VnRaw.rearrange("p (j h e) -> p j h e", j=NSQ, h=4)
        VnT = slotT[28:33]
        accn = sb("accn", [16, 4, 132], F32)
        accnT = Tl()

        gcolS = sb("gcolS", [128, 5, 24], F32)
        gcolT = [Tl() for _ in range(5)]

        GF = ['DecT', 'DecS', 'EGr', 'L', 'T0', 'Na', 'Nb', 'Ta', 'Tb', 'Pa', 'Pb']
        GB = ['IntraT', 'qgT', 'AinvTb', 'kbg', 'kdec', 'vb', 'wTn', 'vnew']
        gtmp = []
        for s_ in range(2):
            d_ = {}
            for nm in GF:
                d_[nm] = (Tl(), sb(f"g{s_}{nm}", [128, 128], F32))
            for nm in GB:
                d_[nm] = (Tl(), sb(f"g{s_}{nm}", [128, 128], BF16))
            gtmp.append(d_)

        def load_x(ch):
            for gi, rg in enumerate(ch.rgs):
                nr = rg.nr
                s0 = 8 * (gi % 2)
                stt = slots(s0, 8)
                sta = slot32(s0, 1024)
                if rg.kind == 'meta':
                    src = meta
                elif rg.kind == 'samp':
                    src = xs[rg.idx]
                else:
                    src = xp[rg.idx * 128:(rg.idx + 1) * 128, :]
                DMA('sp', sta[0:nr, :], src, writes=stt)
                for half in range(2):
                    bk = big.next()
                    for q in range(4):
                        kc = half * 4 + q
                        MM(bankT[bk], banks[bk][:, q * 128:q * 128 + nr], stt, sta[0:nr, kc * 128:(kc + 1) * 128], CT, ident32[0:nr, 0:nr])
                    for q in range(4):
                        kc = half * 4 + q
                        CP('act' if q % 2 else 'dve', xTt[kc], xT[:, kc, rg.col0:rg.col0 + nr], bankT[bk], banks[bk][:, q * 128:q * 128 + nr])

        def store_y(ch):
            for gi, rg in enumerate(ch.rgs):
                if rg.kind == 'meta':
                    continue
                nr = rg.nr
                s0 = 8 * (gi % 2)
                stt = slots(s0, 8)
                sta = slot32(s0, 1024)
                for half in range(2):
                    bk = big.next()
                    for q in range(4):
                        kc = half * 4 + q
                        MM(bankT[bk], banks[bk][0:nr, q * 128:(q + 1) * 128], xTt[kc], xT[:, kc, rg.col0:rg.col0 + nr], CT, ident32[:, :])
                    CP('act' if half else 'dve', stt, sta[0:nr, half * 512:(half + 1) * 512], bankT[bk], banks[bk][0:nr, :])
                dst = ys[rg.idx] if rg.kind == 'samp' else yp[rg.idx * 128:(rg.idx + 1) * 128, :]
                DMA('sp', dst, sta[0:nr, :], reads=stt)

        OSTG = 30

        ostg = Ring([34, 36])

        def tok_out(bk, nr, dst):
            s0 = ostg.next()
            CP('act', slots(s0, 2), slot32(s0, 512)[0:nr, :], bankT[bk], banks[bk][0:nr, :])
            DMA('sp', dst, slot32(s0, 512)[0:nr, :], reads=slots(s0, 2))

        def krow(rg):
            return 0 if rg.kind == 'meta' else NMETA + rg.idx * 128

        def stage_qkv(ch, l):
            n = ch.n
            src = xn_src(n)
            wi = load_strip(w_in[l][:, 0:512], KC, 512)
            for h in range(4):
                bk = proj_feat(wi, h, KC, src, n)
                TS('dve', slotT[h], slot(h, n)[0:64, :], bankT[bk], banks[bk][0:64, 0:n], 0.125, None, ALU.mult)
                MS('pool', slotT[h], slot(h, n)[64:128, :], 0.0)
                ACT(slotT[16 + h], slot(16 + h, n)[64:128, :], bankT[bk], banks[bk][64:128, 0:n], AF.Copy, scale=0.125)
                MS('pool', slotT[16 + h], slot(16 + h, n)[0:64, :], 0.0)
                if ch.kind == 'S':
                    for m in range(2):
                        pr = slice(64 * m, 64 * m + 64)
                        P.op('act', lambda hh, h=h, m=m, pr=pr, bk=bk: hh.activation(
                            out=Qblk[pr, :, h, 8 * m:8 * m + 8],
                            in_=banks[bk][pr, NMETA:NMETA + 32].rearrange("p (j t) -> p j t", t=8),
                            func=AF.Copy, scale=0.125), reads=bankT[bk], writes=QblkT)
            mkq = int(os.environ.get('MK_Q', '9'))
            if mkq < 2:
                return
            wi = load_strip(w_in[l][:, 512:1024], KC, 512)
            for h in range(4):
                bk = proj_feat(wi, h, KC, src, n)
                if ch.kind == 'S':
                    CP('act', KTt[l][0], KTs[l][:, h, 0:NMETA], bankT[bk], banks[bk][:, 0:NMETA])
                    CP('dve', KnTT, KnT[:, h, :], bankT[bk], banks[bk][:, NMETA:NMETA + 32])
                else:
                    c0 = NMETA + ch.pi * CH
                    CP('act', KTt[l][1 + 4 * ch.pi:5 + 4 * ch.pi], KTs[l][:, h, c0:c0 + CH], bankT[bk], banks[bk][:, 0:n])
            if mkq < 3:
                return
            for rg in ch.rgs:
                bk = big.next()
                for kc in range(KC):
                    MM(bankT[bk], banks[bk][0:rg.nr, :], xnTt[kc], xnT[:, kc, rg.col0:rg.col0 + rg.nr], wbufT[wi], wbuf[wi][:, kc, :], start=(kc == 0), stop=(kc == KC - 1))
                if rg.kind == 'samp':
                    tok_out(bk, rg.nr, nks[l, rg.idx])
                else:
                    tok_out(bk, rg.nr, nkp[l, krow(rg):krow(rg) + rg.nr, :])
            if mkq < 4:
                return
            wi = load_strip(w_in[l][:, 1024:1536], KC, 512)
            mkv = int(os.environ.get('MK_V', '9'))
            if ch.kind == 'S' and mkv >= 2:
                MS('pool', VnT, VnRaw[0:8, :], 1.0)
            for rg in ch.rgs:
                bk = big.next()
                for kc in range(KC):
                    MM(bankT[bk], banks[bk][0:rg.nr, :], xnTt[kc], xnT[:, kc, rg.col0:rg.col0 + rg.nr], wbufT[wi], wbuf[wi][:, kc, :], start=(kc == 0), stop=(kc == KC - 1))
                src_v = banks[bk][0:rg.nr, :].rearrange("p (h e) -> p h e", e=128)
                if rg.kind == 'samp':
                    if mkv >= 3:
                        ACT(VnT, Vn[0:rg.nr, rg.idx, :, 0:128], bankT[bk], src_v, AF.Copy)
                    if mkv >= 4:
                        tok_out(bk, rg.nr, nvs[l, rg.idx])
                else:
                    ti = 0 if rg.kind == 'meta' else 1 + rg.idx
                    if mkv >= 3:
                        ACT(Vtt[l][ti], Vts[l][0:rg.nr, ti, :, 0:128], bankT[bk], src_v, AF.Copy)
                    if mkv >= 4:
                        tok_out(bk, rg.nr, nvp[l, krow(rg):krow(rg) + rg.nr, :])

        def attn_finalize(l, accT_, acc0, acc1, den0, den1, nq, h, qc0):
            ai = attring.next()
            a_t, a = attT[ai], att[ai]
            P.op('dve', lambda hh: hh.reciprocal(a[0:nq, 256:257], den0), reads=accT_, writes=a_t)
            P.op('dve', lambda hh: hh.reciprocal(a[0:nq, 257:258], den1), reads=accT_, writes=a_t)
            TT('dve', a_t, a[0:nq, 257:258], a_t, a[0:nq, 257:258], LCT[l], lamv[l][0:nq, 1:2], ALU.mult)
            ACT(a_t, a[0:nq, 0:128], accT_, acc1, AF.Copy, scale=a[0:nq, 257:258], bt=a_t)
            STT('dve', a_t, a[0:nq, 128:256], accT_, acc0, a[0:nq, 256:257], a_t, a[0:nq, 0:128], ALU.mult, ALU.add)
            norm_transpose(a_t, a[0:nq, 128:256], nq, cstA[l][:, 80:81], LCT[l], slotT[20 + h], slot(20 + h)[:, qc0:qc0 + nq])

        def prompt_attention(l, nq, qc0, ktiles):
            for h in range(4):
                bk = big.next()
                acc = banks[bk]
                nkt = len(ktiles)
                for ki, (ti, nk, kc0, diag) in enumerate(ktiles):
                    ht, hap = nexthalf()
                    for m in range(2):
                        qs = h if m == 0 else 16 + h
                        MM(ht, hap[0:nk, m * 128:m * 128 + nq], KTt[l][ti], KTs[l][:, h, kc0:kc0 + nk], slotT[qs], slot(qs)[:, qc0:qc0 + nq])
                    pi_ = ptring.next()
                    pv = PTb[pi_][0:nk, :].rearrange("p (m q) -> p m q", m=2)[:, :, 0:nq]
                    ACT(PTbT[pi_], pv, ht, hap[0:nk, :].rearrange("p (m q) -> p m q", m=2)[:, :, 0:nq], AF.Exp)
                    if diag:
                        for m in range(2):
                            TT('pool', PTbT[pi_], PTb[pi_][0:nk, m * 128:m * 128 + nq], PTbT[pi_], PTb[pi_][0:nk, m * 128:m * 128 + nq], CT, triu_b[0:nk, 0:nq], ALU.mult)
                    for m in range(2):
                        MM(bankT[bk], acc[0:nq, m * 129:(m + 1) * 129], PTbT[pi_], PTb[pi_][0:nk, m * 128:m * 128 + nq], Vtt[l][ti], Vts[l][0:nk, ti, h, 0:129], start=(ki == 0 and m == 0), stop=(ki == nkt - 1), sgc=True)
                attn_finalize(l, bankT[bk], acc[0:nq, 0:128], acc[0:nq, 129:257], acc[0:nq, 128:129], acc[0:nq, 257:258], nq, h, qc0)


        def sample_attention(l, j):
            accA_t = bankT[5]
            accB_t = bankT[4]

            def acc_ap(h):
                if h < 3:
                    return banks[5][0:16, h * 129:(h + 1) * 129]
                return banks[4][0:16, 0:129]

            def acc_t(h):
                return accA_t if h < 3 else accB_t
            for pg in range(NPG):
                bi = pgring.next()

                col = j * NPG + pg
                P.op('pool', lambda hh, bi=bi, col=col: hh.indirect_dma_start(
                    out=kpg[bi][:, :], out_offset=None, in_=ck_rows,
                    in_offset=bass.IndirectOffsetOnAxis(ap=pidx[l][:, col:col + 1], axis=0)), reads=ptsT, writes=kpgT[bi], dma=True)
                P.op('pool', lambda hh, bi=bi, col=col: hh.indirect_dma_start(
                    out=vpg[bi][:, :], out_offset=None, in_=cv_rows,
                    in_offset=bass.IndirectOffsetOnAxis(ap=pidx[l][:, col:col + 1], axis=0)), reads=ptsT, writes=vpgT[bi], dma=True)
                bkT = big.next()
                for h in range(4):
                    MM(bankT[bkT], banks[bkT][:, h * 128:(h + 1) * 128], kpgT[bi], kpg[bi][:, h * 128:(h + 1) * 128], CT, identb[:, :])
                ki_ = kptring.next()
                CP('act' if pg % 2 else 'dve', KpTT[ki_], KpT[ki_][:, :], bankT[bkT], banks[bkT][:, :])
                st_, sa = nextq()
                for h in range(4):
                    MM(st_, sa[:, h * 16:(h + 1) * 16], KpTT[ki_], KpT[ki_][:, h * 128:(h + 1) * 128], QblkT, Qblk[:, j, h, :])
                pi_ = ptsring.next()
                ACT(PTsT[pi_], PTs[pi_][:, :], st_, sa[:, 0:64], AF.Exp)
                for h in range(4):
                    MM(acc_t(h), acc_ap(h)[:, 0:128], PTsT[pi_], PTs[pi_][:, h * 16:(h + 1) * 16], vpgT[bi], vpg[bi][:, h * 128:(h + 1) * 128], start=(pg == 0 and h in (0, 3)), stop=False, sgc=True)
                    MM(acc_t(h), acc_ap(h)[:, 128:129], PTsT[pi_], PTs[pi_][:, h * 16:(h + 1) * 16], CT, ones_b[:, 0:1], start=False, stop=False, sgc=True)
            st_, sa = nextq()
            for h in range(4):
                MM(st_, sa[0:8, h * 16:(h + 1) * 16], KnTT, KnT[:, h, j * 8:(j + 1) * 8], QblkT, Qblk[:, j, h, :])
            pi_ = ptsring.next()
            ACT(PTsT[pi_], PTs[pi_][0:8, :], st_, sa[0:8, 0:64], AF.Exp)
            for h in range(4):
                TT('pool', PTsT[pi_], PTs[pi_][0:8, h * 16:(h + 1) * 16], PTsT[pi_], PTs[pi_][0:8, h * 16:(h + 1) * 16], CT, cm8[:, :], ALU.mult)
            for h in range(4):
                MM(acc_t(h), acc_ap(h), PTsT[pi_], PTs[pi_][0:8, h * 16:(h + 1) * 16], VnT, Vn[0:8, j, h, 0:129], start=False, stop=True, sgc=True)
            for h in range(4):
                P.op('dve', lambda hh, h=h: hh.reciprocal(accn[:, h, 128:129], acc_ap(h)[:, 128:129]), reads=acc_t(h), writes=accnT)
                TT('dve', accnT, accn[:, h, 128:129], accnT, accn[:, h, 128:129], LCT[l], sgn16[l][:, :], ALU.mult)
                TS('dve', accnT, accn[:, h, 0:128], [acc_t(h), accnT], acc_ap(h)[:, 0:128], accn[:, h, 128:129], None, ALU.mult)
                qt_, qa = nextq()
                MM(qt_, qa[0:8, :], CT, sel16[:, :], accnT, accn[:, h, 0:128])
                norm_transpose(qt_, qa[0:8, :], 8, cstA[l][:, 80:81], LCT[l], slotT[20 + h], slot(20 + h)[:, NMETA + 8 * j:NMETA + 8 * j + 8])

        def stage_attention(ch, l):
            if ch.kind == 'S':
                if "2a" not in os.environ.get("MK_SKIP", "").split(","):
                    prompt_attention(l, NMETA, 0, [(0, NMETA, 0, True)])
                if "2b" not in os.environ.get("MK_SKIP", "").split(","):
                    for j in range(NSQ):
                        sample_attention(l, j)
            else:
                for r in range(4):
                    qt_i = 4 * ch.pi + r
                    kts = [(0, NMETA, 0, False)] + [(1 + i, 128, NMETA + 128 * i, i == qt_i) for i in range(qt_i + 1)]
                    prompt_attention(l, 128, 128 * r, kts)

        def conv_block(ch, l, b, bk):
            n = ch.n
            pi_ = prering.next()
            p_t, p = preT[pi_], pre[pi_]
            ci = caccring.next()
            c_t, c = caccT[ci], cacc[ci]
            w = lambda j: cstA[l][:, 32 + 12 * j + b:32 + 12 * j + b + 1]
            if ch.kind == 'P':
                CP('act', p_t, p[:, 3:3 + n], bankT[bk], banks[bk][:, 0:n])
                CP('pool', p_t, p[:, 0:3], halo_gT[l][b], halo_g[l][:, b, :])
                CP('pool', halo_gT[l][b], halo_g[l][:, b, :], p_t, p[:, n:n + 3])
                segs = [(lambda j: p[:, j:j + n], c[:, 0:n])]
            else:
                CP('act', p_t, p[:, 3:19], bankT[bk], banks[bk][:, 0:16])
                MS('pool', p_t, p[:, 0:3], 0.0)
                sv = p[:, 19:63].rearrange("p (j t) -> p j t", t=11)
                CP('act', p_t, sv[:, :, 3:11], bankT[bk], banks[bk][:, 16:48].rearrange("p (j t) -> p j t", t=8))
                CP('pool', p_t, sv[:, :, 0:3], hgsT, hgs[:, b, :, :])
                CP('pool', halo_gT[l][b], halo_g[l][:, b, :], p_t, p[:, 16:19])
                CP('pool', hgoT, hgo[:, b, :, :], p_t, sv[:, :, 8:11])
                segs = [(lambda j: p[:, j:j + 16], c[:, 0:16])]
                for sq_ in range(NSQ):
                    segs.append((lambda j, sq_=sq_: p[:, 19 + 11 * sq_ + j:19 + 11 * sq_ + j + 8], c[:, 16 + 8 * sq_:24 + 8 * sq_]))
            for inf, outv in segs:
                TS('dve', c_t, outv, p_t, inf(3), w(3), None, ALU.mult, st_=LCT[l])
                for j in (2, 1, 0):
                    STT('dve', c_t, outv, p_t, inf(j), w(j), c_t, outv, ALU.mult, ALU.add, st_=LCT[l])
            return ci

        def stage_gdn_proj(ch, l):
            n = ch.n
            src = xn_src(n)
            if ch.kind == 'S':
                for j in range(NSQ):
                    DMA('sp', tokg[3 * j:3 * j + 3, 0:1536], sgc[l, j], writes=tokgT)
                for g4 in range(3):
                    bk = big.next()
                    for q in range(4):
                        b = g4 * 4 + q
                        MM(bankT[bk], banks[bk][:, q * 12:(q + 1) * 12], tokgT, tokg[0:12, b * 128:(b + 1) * 128], CT, ident32[0:12, 0:12])
                    ACT(hgsT, hgs[:, g4 * 4:(g4 + 1) * 4, :, :], bankT[bk], banks[bk][:, 0:48].rearrange("p (b j t) -> p b j t", b=4, j=4), AF.Copy)
            for s_ in range(3):
                wi = load_strip(w_in[l][:, 1536 + 512 * s_:2048 + 512 * s_], KC, 512)
                for blk in range(4):
                    b = s_ * 4 + blk
                    bk = proj_feat(wi, blk, KC, src, n)
                    ci = conv_block(ch, l, b, bk)
                    c_t, c = caccT[ci], cacc[ci]
                    if s_ == 2:
                        ACT(slotT[4 + b], slot(4 + b, n), c_t, c[:, 0:n], AF.Silu)
                    else:
                        yi = ytring.next()
                        ACT(ytmpT[yi], ytmp[yi][:, 0:n], c_t, c[:, 0:n], AF.Silu)
                        si = sqring.next()
                        TT('pool', sqbT[si], sqb[si][:, 0:n], ytmpT[yi], ytmp[yi][:, 0:n], ytmpT[yi], ytmp[yi][:, 0:n], ALU.mult)
                        MM(bankT[3], banks[3][:, 0:n], CT, ones_b[:], sqbT[si], sqb[si][:, 0:n])
                        ri = rsring.next()
                        ACT(rstdT[ri], rstd[ri][:, 0:n], bankT[3], banks[3][:, 0:n], AF.Sqrt, bias=1e-6)
                        P.op('dve', lambda h, ri=ri: h.reciprocal(rstd[ri][:, 0:n], rstd[ri][:, 0:n]), reads=rstdT[ri], writes=rstdT[ri])
                        if s_ == 0:
                            STT('dve', slotT[4 + b], slot(4 + b, n), ytmpT[yi], ytmp[yi][:, 0:n], 128.0 ** -0.5, rstdT[ri], rstd[ri][:, 0:n], ALU.mult, ALU.mult)
                        else:
                            TT('dve', slotT[4 + b], slot(4 + b, n), ytmpT[yi], ytmp[yi][:, 0:n], rstdT[ri], rstd[ri][:, 0:n], ALU.mult)
            wi = load_strip(w_in[l][:, 3072:3584], KC, 512)
            for h in range(4):
                bk = proj_feat(wi, h, KC, src, n)
                ACT(slotT[16 + h], slot(16 + h, n), bankT[bk], banks[bk][:, 0:n], AF.Silu)
            DMA('pool', wba_t[:, :, :], w_in[l][:, 3584:3592].rearrange("(k p) n -> p k n", p=128), writes=wbaT)
            for gi, rg in enumerate(ch.rgs):
                nr = rg.nr
                qt_, qa = nextq()
                for kc in range(KC):
                    MM(qt_, qa[0:nr, 0:8], xnTt[kc], xnT[:, kc, rg.col0:rg.col0 + nr], wbaT, wba_t[:, kc, :], start=(kc == 0), stop=(kc == KC - 1))
                g_t, g = gcolT[gi], gcolS[:, gi, :]
                ACT(g_t, g[0:nr, 0:4], qt_, qa[0:nr, 0:4], AF.Sigmoid)
                TT('dve', g_t, g[0:nr, 20:24], qt_, qa[0:nr, 4:8], LCT[l], dtb[l][0:nr, :], ALU.add)
                ACT(g_t, g[0:nr, 20:24], g_t, g[0:nr, 20:24], AF.Exp)
                ACT(g_t, g[0:nr, 20:24], g_t, g[0:nr, 20:24], AF.Ln, bias=1.0)
                TT('dve', g_t, g[0:nr, 4:8], g_t, g[0:nr, 20:24], LCT[l], negA[l][0:nr, :], ALU.mult)
                qt2, qa2 = nextq()
                MM(qt2, qa2[0:nr, 0:4], CT, tri32[0:nr, 0:nr], g_t, g[0:nr, 4:8])
                CP('dve', g_t, g[0:nr, 8:12], qt2, qa2[0:nr, 0:4])
                TS('dve', g_t, g[0:nr, 12:16], g_t, g[0:nr, 8:12], -1.0, None, ALU.mult)
                ACT(g_t, g[0:nr, 16:20], g_t, g[0:nr, 8:12], AF.Exp)
                TT('dve', g_t, g[0:nr, 16:20], g_t, g[0:nr, 16:20], g_t, g[0:nr, 0:4], ALU.mult)

        def gdn_unit(l, gi, rg, h, S_t, S_ap, Sb_t, Sb_ap):
            C = rg.nr
            c0, c1 = rg.col0, rg.col0 + rg.nr
            G = gtmp[h % 2]
            g_t, g = gcolT[gi], gcolS[:, gi, :]
            qn_t, qn = slotT[4 + h], slot(4 + h)[:, c0:c1]
            kn_t, kn = slotT[8 + h], slot(8 + h)[:, c0:c1]
            vn_t, vn = slotT[12 + h], slot(12 + h)[:, c0:c1]
            K = int(round(math.log2(C))) - 1

            def T(nm):
                return G[nm]
            gbc_t, gbc = T('Na')
            TS('dve', gbc_t, gbc[0:C, :], CT, ones32[0:C, :], g[0:C, 4 + h:5 + h], None, ALU.mult, st_=g_t)
            R_t, R = nextg()
            MM(R_t, R[:, 0:C], gbc_t, gbc[0:C, :], CT, tri32[0:C, 0:C])
            E1_t, E1 = T('DecT')
            TT('dve', E1_t, E1[0:C, 0:C], R_t, R[0:C, 0:C], CT, maskU[0:C, 0:C], ALU.add)
            DT_t, DT = T('DecT')
            ACT(DT_t, DT[0:C, 0:C], E1_t, E1[0:C, 0:C], AF.Exp, bias=g[0:C, 12 + h:13 + h], bt=g_t)
            E2_t, E2 = T('DecS')
            STT('dve', E2_t, E2[0:C, 0:C], R_t, R[0:C, 0:C], -1.0, CT, maskLS[0:C, 0:C], ALU.mult, ALU.add)
            DS_t, DS = T('DecS')
            ACT(DS_t, DS[0:C, 0:C], E2_t, E2[0:C, 0:C], AF.Exp, bias=g[0:C, 8 + h:9 + h], bt=g_t)
            EG_t, EG = T('EGr')
            ACT(EG_t, EG[:, 0:C], R_t, R[:, 0:C], AF.Exp)
            M_t, M = nextg()
            MM(M_t, M[0:C, 0:C], kn_t, kn, kn_t, kn)
            L_t, L = T('L')
            STT('dve', L_t, L[0:C, 0:C], M_t, M[0:C, 0:C], g[0:C, h:h + 1], DS_t, DS[0:C, 0:C], ALU.mult, ALU.mult, st_=g_t)
            QK_t, QK = nextg()
            MM(QK_t, QK[0:C, 0:C], kn_t, kn, qn_t, qn)
            IT_t, IT = T('IntraT')
            TT('dve', IT_t, IT[0:C, 0:C], QK_t, QK[0:C, 0:C], DT_t, DT[0:C, 0:C], ALU.mult)
            qg_t, qg = T('qgT')
            TT('pool', qg_t, qg[:, 0:C], qn_t, qn, EG_t, EG[:, 0:C], ALU.mult)
            t0_t, t0 = nextg()
            MM(t0_t, t0[0:C, 0:C], L_t, L[0:C, 0:C], CT, ident32[0:C, 0:C])
            T0_t, T0 = T('T0')
            CP('act', T0_t, T0[0:C, 0:C], t0_t, t0[0:C, 0:C])
            Pc_t, Pc = T('Pa')
            TT('pool', Pc_t, Pc[0:C, 0:C], CT, ident32[0:C, 0:C], T0_t, T0[0:C, 0:C], ALU.subtract)
            Np_t, Np = L_t, L
            Tp_t, Tp = T0_t, T0
            nbufs = [T('Na'), T('Nb')]
            tbufs = [T('Ta'), T('Tb')]
            pbufs = [T('Pb'), T('Pa')]
            for k in range(1, K + 1):
                n_t, n_ = nextg()
                MM(n_t, n_[0:C, 0:C], Tp_t, Tp[0:C, 0:C], Np_t, Np[0:C, 0:C])
                Nn_t, Nn = nbufs[k % 2]
                CP('act', Nn_t, Nn[0:C, 0:C], n_t, n_[0:C, 0:C])
                if k < K:
                    tt_t, tt_ = nextg()
                    MM(tt_t, tt_[0:C, 0:C], Np_t, Np[0:C, 0:C], Tp_t, Tp[0:C, 0:C])
                    Tn_t, Tn = tbufs[k % 2]
                    CP('dve', Tn_t, Tn[0:C, 0:C], tt_t, tt_[0:C, 0:C])
                p_t, p_ = nextg()
                MM(p_t, p_[0:C, 0:C], Nn_t, Nn[0:C, 0:C], Pc_t, Pc[0:C, 0:C])
                Pn_t, Pn = pbufs[(k - 1) % 2]
                TT('dve', Pn_t, Pn[0:C, 0:C], p_t, p_[0:C, 0:C], Pc_t, Pc[0:C, 0:C], ALU.add)
                Pc_t, Pc = Pn_t, Pn
                Np_t, Np = Nn_t, Nn
                if k < K:
                    Tp_t, Tp = Tn_t, Tn
            Ab_t, Ab = T('AinvTb')
            CP('act', Ab_t, Ab[0:C, 0:C], Pc_t, Pc[0:C, 0:C])
            kt_t, kt = nextg()
            MM(kt_t, kt[0:C, :], kn_t, kn, CT, identb[:, :])
            kbg_t, kbg = T('kbg')
            ACT(kbg_t, kbg[0:C, :], kt_t, kt[0:C, :], AF.Copy, scale=g[0:C, 16 + h:17 + h], bt=g_t)
            kd_t, kd = T('kdec')
            TS('dve', kd_t, kd[0:C, :], kt_t, kt[0:C, :], DT[0:C, C - 1:C], None, ALU.mult, st_=DT_t)
            vt_t, vt = nextg()
            MM(vt_t, vt[0:C, :], vn_t, vn, CT, identb[:, :])
            vb_t, vb = T('vb')
            TS('dve', vb_t, vb[0:C, :], vt_t, vt[0:C, :], g[0:C, h:h + 1], None, ALU.mult, st_=g_t)
            w_t, w_ = nextg()
            MM(w_t, w_[:, 0:C], kbg_t, kbg[0:C, :], Ab_t, Ab[0:C, 0:C])
            wn_t, wn = T('wTn')
            ACT(wn_t, wn[:, 0:C], w_t, w_[:, 0:C], AF.Copy, scale=-1.0)
            vn_ps_t, vn_ps = nextg()
            MM(vn_ps_t, vn_ps[0:C, :], Ab_t, Ab[0:C, 0:C], vb_t, vb[0:C, :], start=True, stop=False)
            MM(vn_ps_t, vn_ps[0:C, :], wn_t, wn[:, 0:C], Sb_t, Sb_ap, start=False, stop=True)
            vw_t, vw = T('vnew')
            CP('act', vw_t, vw[0:C, :], vn_ps_t, vn_ps[0:C, :])
            o_t, o_ = nextg()
            MM(o_t, o_[0:C, :], qg_t, qg[:, 0:C], Sb_t, Sb_ap, start=True, stop=False)
            MM(o_t, o_[0:C, :], IT_t, IT[0:C, 0:C], vw_t, vw[0:C, :], start=False, stop=True)
            norm_transpose(o_t, o_[0:C, :], C, cstA[l][:, 81:82], LCT[l], slotT[24 + h], slot(24 + h)[:, c0:c1],
                           extra=(slotT[16 + h], slot(16 + h)[:, c0:c1]))
            su_t, su = nextg()
            MM(su_t, su[:, :], kd_t, kd[0:C, :], vw_t, vw[0:C, :])
            STT('dve', S_t, S_ap, S_t, S_ap, EG[:, C - 1:C], su_t, su[:, :], ALU.mult, ALU.add, st_=EG_t)
            CP('act', Sb_t, Sb_ap, S_t, S_ap)

        def stage_gdn(ch, l):
            for gi, rg in enumerate(ch.rgs):
                if rg.kind == 'samp':
                    si = 0
                    DMA('sp', SsS[si][:, :, :], sgd[l, rg.idx].rearrange("h d e -> d h e"), writes=SsST[si])
                    for h in range(4):
                        CP('act', SsBT[si][h], SsB[si][:, h, :], SsST[si][h], SsS[si][:, h, :])
                    for h in range(4):
                        gdn_unit(l, gi, rg, h, SsST[si][h], SsS[si][:, h, :], SsBT[si][h], SsB[si][:, h, :])
                    DMA('sp', ngs[l, rg.idx].rearrange("h d e -> d h e"), SsS[si][:, :, :], reads=SsST[si])
                else:
                    for h in range(4):
                        gdn_unit(l, gi, rg, h, SstT[l][h], Sst[l][:, h, :], SbbT[l][h], Sbb[l][:, h, :])

        def stage_mix(ch, l):
            n = ch.n
            src = xn_src(n)
            for half in range(2):
                wi = load_strip(w_in[l][:, 3592 + 512 * half:3592 + 512 * half + 512], KC, 512)
                for blk in range(4):
                    bk = proj_feat(wi, blk, KC, src, n)
                    ACT(slotT[8 + blk], slot(8 + blk, n), bankT[bk], banks[bk][:, 0:n], AF.Sigmoid)
                wi = load_strip(wbra[l][:, 512 * half:512 * half + 512], 4, 512)
                for blk in range(4):
                    ob = half * 4 + blk
                    bk = proj_feat(wi, blk, 4, lambda kc: (slotT[20 + kc], slot(20 + kc, n)), n)
                    yt, ya = slots(28 + 2 * blk, 2), slot32(28 + 2 * blk, n)
                    TT('dve', yt, ya, bankT[bk], banks[bk][:, 0:n], slotT[8 + blk], slot(8 + blk, n), ALU.mult)
                wi = load_strip(w_in[l][:, 4616 + 512 * half:4616 + 512 * half + 512], KC, 512)
                for blk in range(4):
                    bk = proj_feat(wi, blk, KC, src, n)
                    ACT(slotT[12 + blk], slot(12 + blk, n), bankT[bk], banks[bk][:, 0:n], AF.Sigmoid)
                wi = load_strip(wbrb[l][:, 512 * half:512 * half + 512], 4, 512)
                for blk in range(4):
                    ob = half * 4 + blk
                    bk = proj_feat(wi, blk, 4, lambda kc: (slotT[24 + kc], slot(24 + kc, n)), n)
                    yi = ytring.next()
                    TT('dve', ytmpT[yi], ytmp[yi][:, 0:n], bankT[bk], banks[bk][:, 0:n], slotT[12 + blk], slot(12 + blk, n), ALU.mult)
                    yt, ya = slots(28 + 2 * blk, 2), slot32(28 + 2 * blk, n)
                    TT('pool', slotT[ob], slot(ob, n), ytmpT[yi], ytmp[yi][:, 0:n], yt, ya, ALU.add)
            for half in range(2):
                wi = load_strip(w_out[l][:, 512 * half:512 * half + 512], KC, 512)
                for blk in range(4):
                    ob = half * 4 + blk
                    bk = proj_feat(wi, blk, KC, lambda kc: (slotT[kc], slot(kc, n)), n)
                    yt, ya = ybuf(ob, n)
                    CP('act', yt, ya, bankT[bk], banks[bk][:, 0:n])
            postnorm_residual(l, 1, n)

        gbuf = pre

        def ffn_conv_block(ch, l, j, bk):
            n = ch.n
            pi_ = prering.next()
            p_t, p = preT[pi_], gbuf[pi_]
            ci = caccring.next()
            c_t, c = caccT[ci], cacc[ci]
            w = lambda t: cstB[l][:, 22 * t + j:22 * t + j + 1]
            bcol = cstA[l][:, 82 + j:83 + j]
            if ch.kind == 'P':
                CP('act', p_t, p[:, 2:2 + n], bankT[bk], banks[bk][:, 0:n])
                CP('pool', p_t, p[:, 0:2], halo_fT[l][j], halo_f[l][:, j, :])
                CP('pool', halo_fT[l][j], halo_f[l][:, j, :], p_t, p[:, n:n + 2])
                segs = [(lambda t: p[:, t:t + n], c[:, 0:n])]
            else:
                CP('act', p_t, p[:, 2:18], bankT[bk], banks[bk][:, 0:16])
                MS('pool', p_t, p[:, 0:2], 0.0)
                sv = p[:, 18:58].rearrange("p (j t) -> p j t", t=10)
                CP('act', p_t, sv[:, :, 2:10], bankT[bk], banks[bk][:, 16:48].rearrange("p (j t) -> p j t", t=8))
                CP('pool', p_t, sv[:, :, 0:2], hfsT, hfs[:, j, :, :])
                CP('pool', halo_fT[l][j], halo_f[l][:, j, :], p_t, p[:, 16:18])
                CP('pool', hfoT, hfo[:, j, :, :], p_t, sv[:, :, 8:10])
                segs = [(lambda t: p[:, t:t + 16], c[:, 0:16])]
                for sq_ in range(NSQ):
                    segs.append((lambda t, sq_=sq_: p[:, 18 + 10 * sq_ + t:18 + 10 * sq_ + t + 8], c[:, 16 + 8 * sq_:24 + 8 * sq_]))
            for inf, outv in segs:
                TS('dve', c_t, outv, p_t, inf(2), w(2), bcol, ALU.mult, ALU.add, st_=LCT[l])
                for t in (1, 0):
                    STT('dve', c_t, outv, p_t, inf(t), w(t), c_t, outv, ALU.mult, ALU.add, st_=LCT[l])
            return ci

        def stage_ffn(ch, l):
            n = ch.n
            prenorm(l, 2, n)
            src = xn_src(n)
            if ch.kind == 'S':
                for j in range(NSQ):
                    DMA('sp', tokst[2 * j:2 * j + 2, 0:DFF], sfc[l, j], writes=tokstT)
                for g4 in range(6):
                    bk = big.next()
                    nb = 4 if g4 < 5 else 2
                    for q in range(nb):
                        b = g4 * 4 + q
                        MM(bankT[bk], banks[bk][:, q * 8:(q + 1) * 8], tokstT, tokst[0:8, b * 128:(b + 1) * 128], CT, ident32[0:8, 0:8])
                    ACT(hfsT, hfs[:, g4 * 4:g4 * 4 + nb, :, :], bankT[bk], banks[bk][:, 0:8 * nb].rearrange("p (b j t) -> p b j t", b=nb, j=4), AF.Copy)
            for s_ in range(6):
                nb = 4 if s_ < 5 else 2
                wu = load_strip(w_ffn[l][:, 512 * s_:512 * s_ + 128 * nb], KC, 128 * nb)
                wg = load_strip(w_ffn[l][:, DFF + 512 * s_:DFF + 512 * s_ + 128 * nb], KC, 128 * nb)
                for blk in range(nb):
                    j = s_ * 4 + blk
                    bkg = proj_feat(wg, blk, KC, src, n)
                    ci = ffn_conv_block(ch, l, j, bkg)
                    c_t, c = caccT[ci], cacc[ci]
                    yi = ytring.next()
                    y_t, y = ytmpT[yi], ytmp[yi]
                    ACT(y_t, y[:, 0:n], c_t, c[:, 0:n], AF.Gelu_apprx_tanh)
                    bku = proj_feat(wu, blk, KC, src, n)
                    TT('dve', slotT[j], slot(j, n), bankT[bku], banks[bku][:, 0:n], y_t, y[:, 0:n], ALU.mult)
            pieces = [(0, 8), (8, 8), (16, 6)]
            for half in range(2):
                for pi_, (k0, nk) in enumerate(pieces):
                    wi = load_strip(w_dn[l][k0 * 128:(k0 + nk) * 128, 512 * half:512 * half + 512], nk, 512)
                    for blk in range(4):
                        for kc in range(nk):
                            MM(bankT[blk], banks[blk][:, 0:n], wbufT[wi], wbuf[wi][:, kc, blk * 128:(blk + 1) * 128], slotT[k0 + kc], slot(k0 + kc, n),
                               start=(pi_ == 0 and kc == 0), stop=(pi_ == 2 and kc == nk - 1))
                for blk in range(4):
                    ob = half * 4 + blk
                    yt, ya = ybuf(ob, n)
                    CP('act', yt, ya, bankT[blk], banks[blk][:, 0:n])
            postnorm_residual(l, 3, n)

        def feat2tok_out(src_fn, nb, r, dst2d, src_t):
            ngrp = (nb + 3) // 4
            for g4 in range(ngrp):
                bk = big.next()
                k = min(4, nb - 4 * g4)
                for q in range(k):
                    MM(bankT[bk], banks[bk][0:r, q * 128:(q + 1) * 128], src_t, src_fn(g4 * 4 + q), CT, ident32[:, :])
                CP('dve', tokstT, tokst[0:r, g4 * 512:g4 * 512 + 128 * k], bankT[bk], banks[bk][0:r, 0:128 * k])
            DMA('sp', dst2d, tokst[0:r, 0:nb * 128], reads=tokstT)

        chunks = [Chunk('S', 0)] + [Chunk('P', i) for i in range(NPCH)]
        nstage = 0
        done = False
        for ch in chunks:
            for l in range(2):
                if done:
                    break
                sub = int(os.environ.get("MK_SUB", "99"))
                if l == 0:
                    load_x(ch)
                prenorm(l, 0, ch.n)
                if sub >= 1:
                    stage_qkv(ch, l)
                skip = os.environ.get("MK_SKIP", "").split(",")
                if sub >= 2 and "2" not in skip:
                    stage_attention(ch, l)
                if sub >= 3:
                    stage_gdn_proj(ch, l)
                if sub >= 4 and "4" not in skip:
                    stage_gdn(ch, l)
                if sub >= 5:
                    stage_mix(ch, l)
                if sub >= 6:
                    stage_ffn(ch, l)
                if ch.kind == 'S':
                    for j in range(NSQ):
                        feat2tok_out(lambda b, j=j: hgo[:, b, j, :], 12, 3, ncs[l, j], hgoT)
                        feat2tok_out(lambda b, j=j: hfo[:, b, j, :], NFB, 2, nfs[l, j], hfoT)
                if ch.kind == 'P' and ch.pi == NPCH - 1:
                    DMA('sp', ngp[l].rearrange("h d e -> d h e"), Sst[l][:, :, :], reads=SstT[l])
                    feat2tok_out(lambda b, l=l: halo_g[l][:, b, :], 12, 3, ncp[l], halo_gT[l])
                    feat2tok_out(lambda b, l=l: halo_f[l][:, b, :], NFB, 2, nfp[l], halo_fT[l])
                if l == 1:
                    store_y(ch)
                nstage += 1
                if stop_after is not None and nstage >= stop_after:
                    done = True
        P.emit(st)
    return nc


_NC_CACHE = {}


def kernel(x_prompt, x_sample, cache_k, cache_v, page_table, state_gdn, state_gdn_conv, state_ffn_conv,
           meta_tokens, ln_mix_pre, ln_mix_post, ln_ffn_pre, ln_ffn_post, w_in,
           da_lambda_q1, da_lambda_k1, da_lambda_q2, da_lambda_k2, da_subln,
           gdn_conv_w, gdn_a_log, gdn_dt_bias, gdn_norm, w_branch_a, w_branch_b, w_out,
           w_ffn_in, ffn_conv_w, ffn_conv_b, w_down):
    f = lambda a: np.ascontiguousarray(np.asarray(a, dtype=np.float32))
    stop_after = os.environ.get("MK_STOP")
    stop_after = int(stop_after) if stop_after else None
    npool = int(np.asarray(cache_k).shape[1])
    nc = build(stop_after, npool)
    ck = f(cache_k).reshape(2, npool, 128, 512)
    cv = f(cache_v).reshape(2, npool, 128, 512)
    pt = np.ascontiguousarray(np.asarray(page_table, dtype=np.int32))
    shared = dict(
        ck=ck, cv=cv, meta=f(meta_tokens), ln_mix_pre=f(ln_mix_pre), ln_mix_post=f(ln_mix_post),
        ln_ffn_pre=f(ln_ffn_pre), ln_ffn_post=f(ln_ffn_post), w_in=f(w_in),
        lq1=f(da_lambda_q1), lk1=f(da_lambda_k1), lq2=f(da_lambda_q2), lk2=f(da_lambda_k2),
        da_subln=f(da_subln), gdn_conv_w=f(gdn_conv_w), gdn_a_log=f(gdn_a_log), gdn_dt_bias=f(gdn_dt_bias),
        gdn_norm=f(gdn_norm), w_branch_a=f(w_branch_a), w_branch_b=f(w_branch_b), w_out=f(w_out),
        w_ffn_in=f(w_ffn_in), ffn_conv_w=f(ffn_conv_w), ffn_conv_b=f(ffn_conv_b), w_down=f(w_down))
    xpf = f(x_prompt)
    xsf = f(x_sample)
    sg = f(state_gdn)
    sgc = f(state_gdn_conv)
    sfc = f(state_ffn_conv)
    in_maps = []
    for c in range(NCORE):
        m = dict(shared)
        b0 = NSQ * c
        m["xp"] = xpf[c]
        m["xs"] = np.ascontiguousarray(xsf[b0:b0 + NSQ])
        m["pt"] = np.ascontiguousarray(pt[b0:b0 + NSQ].reshape(1, NSQ * NPG))
        m["sg"] = np.ascontiguousarray(sg[:, b0:b0 + NSQ])
        m["sgc"] = np.ascontiguousarray(sgc[:, b0:b0 + NSQ])
        m["sfc"] = np.ascontiguousarray(sfc[:, b0:b0 + NSQ])
        in_maps.append(m)
    res = run_bass_kernel_spmd(nc, in_maps, core_ids=list(range(NCORE)))
    R = res.results
    y_prompt = np.stack([R[c]["yp"] for c in range(NCORE)], 0)
    y_sample = np.concatenate([R[c]["ys"] for c in range(NCORE)], 0)
    nkp = np.stack([R[c]["nkp"] for c in range(NCORE)], 1).reshape(2, NCORE, NMETA + SEQ, 4, 2, 64)
    nvp = np.stack([R[c]["nvp"] for c in range(NCORE)], 1).reshape(2, NCORE, NMETA + SEQ, 4, 128)
    nks = np.concatenate([R[c]["nks"] for c in range(NCORE)], 1).reshape(2, NCORE * NSQ, DSEQ, 4, 2, 64)
    nvs = np.concatenate([R[c]["nvs"] for c in range(NCORE)], 1).reshape(2, NCORE * NSQ, DSEQ, 4, 128)
    ngp = np.stack([R[c]["ngp"] for c in range(NCORE)], 1)
    ngs = np.concatenate([R[c]["ngs"] for c in range(NCORE)], 1)
    ncp = np.stack([R[c]["ncp"] for c in range(NCORE)], 1)
    ncs = np.concatenate([R[c]["ncs"] for c in range(NCORE)], 1)
    nfp = np.stack([R[c]["nfp"] for c in range(NCORE)], 1)
    nfs = np.concatenate([R[c]["nfs"] for c in range(NCORE)], 1)
    outs = (y_prompt, y_sample, nkp, nvp, nks, nvs, ngp, ngs, ncp, ncs, nfp, nfs)
    return tuple(np.ascontiguousarray(o.astype(np.float32)) for o in outs)
```
